# Optimizing a Trainium2 kernel written in Bass

```python
import jax, jax.numpy as jnp
from jax import lax
import numpy as np

D_MODEL = 1024
BATCH = 4
SEQ = 8192
DEPTH = 4

GRID_W = 64
CTX_LEN = 256

D_FF = 2816

GLA_HEADS = 4
GLA_DK = 64
GLA_DV = 128
GLA_QK = GLA_HEADS * GLA_DK
GLA_V = GLA_HEADS * GLA_DV
GLA_RANK = 16
GLA_TAU = 16.0
GLA_CHUNK = 64

MLA_HEADS = 4
MLA_D_NOPE = 128
MLA_D_ROPE = 64
MLA_D_V = 128
MLA_D_CQ = 384
MLA_D_CKV = 128
MLA_SCALE = (MLA_D_NOPE + MLA_D_ROPE) ** -0.5

NA_HEADS = 16
NA_HEAD_DIM = 64
NA_KH = 8
NA_KW = 16

ROPE_BASE = 10000.0
Q_BLOCK = 128
EPS = 1e-6
N_EVEN = (DEPTH + 1) // 2
N_ODD = DEPTH // 2
ALPHA = (2 * DEPTH) ** 0.25
BETA = (8 * DEPTH) ** -0.25

EVEN_SIZES = (GLA_QK, GLA_QK, GLA_V, GLA_V, GLA_RANK, GLA_RANK, MLA_D_CQ, MLA_D_CKV, MLA_D_ROPE)
EVEN_OUT_IN = GLA_V + MLA_HEADS * MLA_D_V
NA_WIDTH = NA_HEADS * NA_HEAD_DIM

kernel_name = 'hybrid_gla_mla_natten_macaron_deepnorm'


def layer_norm(x):
    xf = x.astype(jnp.float32)
    mu = jnp.mean(xf, axis=-1, keepdims=True)
    var = jnp.mean(jnp.square(xf - mu), axis=-1, keepdims=True)
    return ((xf - mu) * lax.rsqrt(var + EPS)).astype(x.dtype)


def rms_norm(x, g):
    xf = x.astype(jnp.float32)
    y = xf * lax.rsqrt(jnp.mean(jnp.square(xf), axis=-1, keepdims=True) + EPS)
    return y.astype(x.dtype) * g


def modulate(x, shift, scale):
    return x * (1.0 + scale) + shift


def post_norm(x, y):
    return layer_norm(ALPHA * x + y)


def swiglu(h, w_in, w_out):
    gate, up = jnp.split(h @ w_in, 2, axis=-1)
    return (jax.nn.silu(gate) * up) @ w_out


def axial_rope(n_tokens):
    t = jnp.arange(n_tokens)
    row = (t // GRID_W).astype(jnp.float32)
    col = (t % GRID_W).astype(jnp.float32)
    n_freq = MLA_D_ROPE // 4
    inv = ROPE_BASE ** (-jnp.arange(n_freq, dtype=jnp.float32) / n_freq)
    ang = jnp.concatenate([row[:, None] * inv, col[:, None] * inv], axis=-1)
    return jnp.cos(ang), jnp.sin(ang)


def apply_rope(x, cos, sin):
    x2 = x.reshape(*x.shape[:-1], -1, 2).astype(jnp.float32)
    x0, x1 = x2[..., 0], x2[..., 1]
    out = jnp.stack([x0 * cos - x1 * sin, x0 * sin + x1 * cos], axis=-1)
    return out.reshape(x.shape).astype(x.dtype)


def dense_attention(q, k, v, scale):
    B, T, H, dk = q.shape
    nb = T // Q_BLOCK
    qb = q.reshape(B, nb, Q_BLOCK, H, dk).transpose(1, 0, 2, 3, 4)

    def block(qi):
        s = jnp.einsum('bqhd,bkhd->bhqk', qi, k).astype(jnp.float32) * scale
        p = jax.nn.softmax(s, axis=-1).astype(v.dtype)
        return jnp.einsum('bhqk,bkhd->bqhd', p, v)

    o = lax.map(block, qb)
    return o.transpose(1, 0, 2, 3, 4).reshape(B, T, H, v.shape[-1])


def gla_chunked(q, k, v, log_g, s0, include_diag):
    B, T, H, DK = q.shape
    DV = v.shape[-1]
    n = T // GLA_CHUNK

    def to_chunks(a):
        return a.reshape(B, n, GLA_CHUNK, H, a.shape[-1]).transpose(1, 0, 3, 2, 4)

    idx = jnp.arange(GLA_CHUNK)
    mask = (idx[:, None] >= idx[None, :]) if include_diag else (idx[:, None] > idx[None, :])

    def step(s, inp):
        qi, ki, vi, gi = inp
        b = jnp.cumsum(gi.astype(jnp.float32), axis=-2)
        diff = b[:, :, :, None, :] - b[:, :, None, :, :]
        decay = jnp.exp(jnp.where(mask[:, :, None], diff, -jnp.inf)).astype(qi.dtype)
        att = jnp.einsum('bhid,bhjd,bhijd->bhij', qi, ki, decay)
        o_intra = jnp.einsum('bhij,bhjv->bhiv', att, vi)
        o_inter = jnp.einsum('bhid,bhdv->bhiv', qi * jnp.exp(b).astype(qi.dtype), s)
        b_last = b[:, :, -1:, :]
        k_dec = ki * jnp.exp(b_last - b).astype(ki.dtype)
        s_new = s * jnp.exp(b_last[:, :, 0, :])[..., None].astype(s.dtype) + jnp.einsum('bhjd,bhjv->bhdv', k_dec, vi)
        return s_new, o_intra + o_inter

    s_fin, o = lax.scan(step, s0, (to_chunks(q), to_chunks(k), to_chunks(v), to_chunks(log_g)))
    return o.transpose(1, 0, 3, 2, 4).reshape(B, T, H, DV), s_fin


def gla_log_gate(lr, wg2, bg):
    return jax.nn.log_sigmoid((lr @ wg2 + bg).astype(jnp.float32)) / GLA_TAU


def gla_out(o, r, norm_g):
    B, T = o.shape[:2]
    return rms_norm(o, norm_g).reshape(B, T, GLA_V) * jax.nn.silu(r)


def even_project(h, w_in, wg2_f, bg_f, wg2_b, bg_b, q_norm_g, kv_norm_g, w_uq, w_ukv):
    B, T, _ = h.shape
    split_points = np.cumsum(EVEN_SIZES)[:-1].tolist()
    q_g, k_g, v_g, r_g, lr_f, lr_b, c_q, c_kv, k_r = jnp.split(h @ w_in, split_points, axis=-1)
    heads = lambda a, n: a.reshape(B, T, n, -1)
    gla = (heads(q_g, GLA_HEADS) * GLA_DK ** -0.5, heads(k_g, GLA_HEADS), heads(v_g, GLA_HEADS),
           heads(gla_log_gate(lr_f, wg2_f, bg_f), GLA_HEADS), heads(gla_log_gate(lr_b, wg2_b, bg_b), GLA_HEADS), r_g)
    q = heads(rms_norm(c_q, q_norm_g) @ w_uq, MLA_HEADS)
    kv = heads(rms_norm(c_kv, kv_norm_g) @ w_ukv, MLA_HEADS)
    mla = (q[..., :MLA_D_NOPE], q[..., MLA_D_NOPE:], kv[..., :MLA_D_NOPE], k_r, kv[..., MLA_D_NOPE:])
    return gla, mla


def even_mixer(hc, hl, cos, sin, w_in, wg2_f, bg_f, wg2_b, bg_b, gla_norm_g, q_norm_g, kv_norm_g,
               w_uq, w_ukv, w_out, with_ctx_out):
    B, S, _ = hl.shape
    args = (w_in, wg2_f, bg_f, wg2_b, bg_b, q_norm_g, kv_norm_g, w_uq, w_ukv)
    (gq_c, gk_c, gv_c, gf_c, gb_c, r_c), (qn_c, qr_c, kn_c, kr_c, v_c) = even_project(hc, *args)
    (gq_l, gk_l, gv_l, gf_l, gb_l, r_l), (qn_l, qr_l, kn_l, kr_l, v_l) = even_project(hl, *args)

    s0 = jnp.zeros((B, GLA_HEADS, GLA_DK, GLA_DV), hl.dtype)
    flip = lambda a: a[:, ::-1]
    o_cf, s_cf = gla_chunked(gq_c, gk_c, gv_c, gf_c, s0, True)
    o_cb, s_cb = gla_chunked(flip(gq_c), flip(gk_c), flip(gv_c), flip(gb_c), s0, False)
    o_lf, _ = gla_chunked(gq_l, gk_l, gv_l, gf_l, s_cf, True)
    o_lb, _ = gla_chunked(flip(gq_l), flip(gk_l), flip(gv_l), flip(gb_l), s_cb, False)
    gla_l = gla_out(o_lf + flip(o_lb), r_l, gla_norm_g)

    rope_shape = kn_l.shape[:-1] + (MLA_D_ROPE,)
    q_l = jnp.concatenate([qn_l, apply_rope(qr_l, cos[None, :, None], sin[None, :, None])], axis=-1)
    kr_l_rot = apply_rope(kr_l, cos[None], sin[None])
    k_l = jnp.concatenate([kn_l, jnp.broadcast_to(kr_l_rot[:, :, None], rope_shape)], axis=-1)
    k_c = jnp.concatenate([kn_c, jnp.broadcast_to(kr_c[:, :, None], kn_c.shape[:-1] + (MLA_D_ROPE,))], axis=-1)
    k_all = jnp.concatenate([k_c, k_l], axis=1)
    v_all = jnp.concatenate([v_c, v_l], axis=1)
    mla_l = dense_attention(q_l, k_all, v_all, MLA_SCALE).reshape(B, S, -1)

    yl = jnp.concatenate([gla_l, mla_l], axis=-1) @ w_out
    yc = None
    if with_ctx_out:
        Tc = hc.shape[1]
        gla_c = gla_out(o_cf + flip(o_cb), r_c, gla_norm_g)
        q_c = jnp.concatenate([qn_c, qr_c], axis=-1)
        mla_c = dense_attention(q_c, k_c, v_c, MLA_SCALE).reshape(B, Tc, -1)
        yc = jnp.concatenate([gla_c, mla_c], axis=-1) @ w_out
    return yc, yl


def neighbourhood_attention(q, k, v, k_ctx, v_ctx, rpb):
    B, S, H, d = q.shape
    rows = S // GRID_W
    kh = min(NA_KH, rows)
    scale = d ** -0.5
    qg = q.reshape(B, rows, GRID_W, H, d)
    kg = k.reshape(B, rows, GRID_W, H, d)
    vg = v.reshape(B, rows, GRID_W, H, d)
    col = jnp.arange(GRID_W)
    col_start = jnp.clip(col - NA_KW // 2, 0, GRID_W - NA_KW)
    col_idx = col_start[:, None] + jnp.arange(NA_KW)[None, :]
    bias_col = rpb[:, :, col_idx - col[:, None] + (NA_KW - 1)]

    def row_block(r):
        r_start = jnp.clip(r - kh // 2, 0, rows - kh)
        k_win = lax.dynamic_slice_in_dim(kg, r_start, kh, axis=1)[:, :, col_idx]
        v_win = lax.dynamic_slice_in_dim(vg, r_start, kh, axis=1)[:, :, col_idx]
        q_r = lax.dynamic_index_in_dim(qg, r, axis=1, keepdims=False)
        rel_row = r_start + jnp.arange(kh) - r + (NA_KH - 1)
        bias = jnp.take(bias_col, rel_row, axis=1).transpose(0, 2, 1, 3)
        s_nb = jnp.einsum('bchd,brcwhd->bhcrw', q_r, k_win).astype(jnp.float32) * scale + bias[None]
        s_cx = jnp.einsum('bchd,bkhd->bhck', q_r, k_ctx).astype(jnp.float32) * scale
        p = jax.nn.softmax(jnp.concatenate([s_nb.reshape(B, H, GRID_W, kh * NA_KW), s_cx], axis=-1), axis=-1)
        p = p.astype(v.dtype)
        p_nb = p[..., :kh * NA_KW].reshape(B, H, GRID_W, kh, NA_KW)
        p_cx = p[..., kh * NA_KW:]
        return (jnp.einsum('bhcrw,brcwhd->bchd', p_nb, v_win)
                + jnp.einsum('bhck,bkhd->bchd', p_cx, v_ctx))

    o = lax.map(row_block, jnp.arange(rows))
    return o.transpose(1, 0, 2, 3, 4).reshape(B, S, H, d)


def odd_mixer(hc, hl, w_in, rpb, w_out, with_ctx_out):
    def project(h):
        B, T, _ = h.shape
        q, k, v = jnp.split(h @ w_in, 3, axis=-1)
        return (q.reshape(B, T, NA_HEADS, NA_HEAD_DIM), k.reshape(B, T, NA_HEADS, NA_HEAD_DIM),
                v.reshape(B, T, NA_HEADS, NA_HEAD_DIM))

    B, S, _ = hl.shape
    q_c, k_c, v_c = project(hc)
    q_l, k_l, v_l = project(hl)
    yl = neighbourhood_attention(q_l, k_l, v_l, k_c, v_c, rpb).reshape(B, S, NA_WIDTH) @ w_out
    yc = None
    if with_ctx_out:
        o_c = dense_attention(q_c, k_c, v_c, NA_HEAD_DIM ** -0.5)
        yc = o_c.reshape(B, hc.shape[1], NA_WIDTH) @ w_out
    return yc, yl


def setup_inputs(seed: int = 0) -> dict:
    key = jax.random.key(seed)
    it = iter(list(jax.random.split(key, 32)))
    nrm = lambda shape, scale: jax.random.normal(next(it), shape, jnp.float32) * scale
    D = D_MODEL
    gate_base = jnp.tile(jnp.concatenate([jnp.zeros((2 * D,), jnp.float32), jnp.ones((D,), jnp.float32)]), 3)
    return {
        'x': nrm((BATCH, SEQ, D), 1.0),
        'c': nrm((BATCH, D), 1.0),
        'ctx': nrm((BATCH, CTX_LEN, D), 1.0),
        'c_ctx': nrm((D,), 1.0),
        'ada_w': nrm((DEPTH, D, 9 * D), 0.5 * D ** -0.5),
        'ada_b': gate_base + nrm((DEPTH, 9 * D), 0.02),
        'ffn1_w_in': nrm((DEPTH, D, 2 * D_FF), D ** -0.5),
        'ffn1_w_out': nrm((DEPTH, D_FF, D), BETA * D_FF ** -0.5),
        'ffn2_w_in': nrm((DEPTH, D, 2 * D_FF), D ** -0.5),
        'ffn2_w_out': nrm((DEPTH, D_FF, D), BETA * D_FF ** -0.5),
        'even_w_in': nrm((N_EVEN, D, sum(EVEN_SIZES)), D ** -0.5),
        'gla_wg2_f': nrm((N_EVEN, GLA_RANK, GLA_QK), GLA_RANK ** -0.5),
        'gla_bg_f': nrm((N_EVEN, GLA_QK), 0.5),
        'gla_wg2_b': nrm((N_EVEN, GLA_RANK, GLA_QK), GLA_RANK ** -0.5),
        'gla_bg_b': nrm((N_EVEN, GLA_QK), 0.5),
        'gla_norm_g': 1.0 + nrm((N_EVEN, GLA_DV), 0.02),
        'mla_q_norm_g': 1.0 + nrm((N_EVEN, MLA_D_CQ), 0.02),
        'mla_kv_norm_g': 1.0 + nrm((N_EVEN, MLA_D_CKV), 0.02),
        'mla_w_uq': nrm((N_EVEN, MLA_D_CQ, MLA_HEADS * (MLA_D_NOPE + MLA_D_ROPE)), MLA_D_CQ ** -0.5),
        'mla_w_ukv': nrm((N_EVEN, MLA_D_CKV, MLA_HEADS * (MLA_D_NOPE + MLA_D_V)), MLA_D_CKV ** -0.5),
        'even_w_out': nrm((N_EVEN, EVEN_OUT_IN, D), BETA * EVEN_OUT_IN ** -0.5),
        'na_w_in': nrm((N_ODD, D, 3 * NA_WIDTH), D ** -0.5),
        'na_rpb': nrm((N_ODD, NA_HEADS, 2 * NA_KH - 1, 2 * NA_KW - 1), 0.5),
        'na_w_out': nrm((N_ODD, NA_WIDTH, D), BETA * NA_WIDTH ** -0.5),
    }


def reference(x, c, ctx, c_ctx, ada_w, ada_b, ffn1_w_in, ffn1_w_out, ffn2_w_in, ffn2_w_out,
              even_w_in, gla_wg2_f, gla_bg_f, gla_wg2_b, gla_bg_b, gla_norm_g, mla_q_norm_g, mla_kv_norm_g,
              mla_w_uq, mla_w_ukv, even_w_out, na_w_in, na_rpb, na_w_out):
    S = x.shape[1]
    cos, sin = axial_rope(S)
    xl, xc = x, ctx
    for l in range(DEPTH):
        last = l == DEPTH - 1
        i = l // 2
        mod_l = jax.nn.silu(c) @ ada_w[l] + ada_b[l]
        mod_c = jax.nn.silu(c_ctx) @ ada_w[l] + ada_b[l]
        sh1, sc1, g1, sh2, sc2, g2, sh3, sc3, g3 = [m[:, None, :] for m in jnp.split(mod_l, 9, axis=-1)]
        csh1, csc1, cg1, csh2, csc2, cg2, csh3, csc3, cg3 = jnp.split(mod_c, 9)

        xl = post_norm(xl, 0.5 * g1 * swiglu(modulate(xl, sh1, sc1), ffn1_w_in[l], ffn1_w_out[l]))
        xc = post_norm(xc, 0.5 * cg1 * swiglu(modulate(xc, csh1, csc1), ffn1_w_in[l], ffn1_w_out[l]))

        hl = modulate(xl, sh2, sc2)
        hc = modulate(xc, csh2, csc2)
        if l % 2 == 0:
            yc, yl = even_mixer(hc, hl, cos, sin, even_w_in[i], gla_wg2_f[i], gla_bg_f[i], gla_wg2_b[i], gla_bg_b[i],
                                gla_norm_g[i], mla_q_norm_g[i], mla_kv_norm_g[i], mla_w_uq[i], mla_w_ukv[i],
                                even_w_out[i], not last)
        else:
            yc, yl = odd_mixer(hc, hl, na_w_in[i], na_rpb[i], na_w_out[i], not last)
        xl = post_norm(xl, g2 * yl)

        xl = post_norm(xl, 0.5 * g3 * swiglu(modulate(xl, sh3, sc3), ffn2_w_in[l], ffn2_w_out[l]))
        if not last:
            xc = post_norm(xc, cg2 * yc)
            xc = post_norm(xc, 0.5 * cg3 * swiglu(modulate(xc, csh3, csc3), ffn2_w_in[l], ffn2_w_out[l]))
    return xl
```

```python
from contextlib import ExitStack
import numpy as np
import concourse.bass as bass
import concourse.mybir as mybir
from concourse.bass_utils import run_bass_kernel_spmd

F32 = mybir.dt.float32
BF16 = mybir.dt.bfloat16
AF = mybir.ActivationFunctionType
ALU = mybir.AluOpType
AX = mybir.AxisListType

D = 1024
FF = 2816
KC = 8
FC = 22
GRID_W = 64
EPS = 1e-6
NDMA_SEM = 8
SAME_ENGINE_SYNC = True
STREAMS = ("pe", "act", "dve", "pool", "sp")


class Prog:
    def __init__(self, nc, es):
        self.nc = nc
        self.csem = {}
        self.ccount = {}
        for s in ("pe", "act", "dve", "pool"):
            self.csem[s] = es.enter_context(nc.semaphore("c_" + s))
            self.ccount[s] = 0
        self.dsem = {}
        self.dcount = {}
        for s in ("sp", "act", "pool"):
            self.dsem[s] = [es.enter_context(nc.semaphore("d_%s%d" % (s, i))) for i in range(NDMA_SEM)]
            self.dcount[s] = 0
        self.n_instr = 0


class Op:
    __slots__ = ("stream", "fn", "dma", "deps", "token", "signal", "idx")


class Phase:
    def __init__(self, prog):
        self.prog = prog
        self.ops = []
        self.res = {}

    def _add(self, stream, fn, reads, writes, dma):
        op = Op()
        op.stream = stream
        op.fn = fn
        op.dma = dma
        op.idx = len(self.ops)
        op.signal = dma
        op.token = None
        deps = set()
        for k in reads:
            st = self.res.get(k)
            if st is not None and st[0] is not None:
                deps.add(st[0])
        for k in writes:
            st = self.res.get(k)
            if st is not None:
                if st[0] is not None:
                    deps.add(st[0])
                deps.update(st[1].values())
                deps.update(st[2])
        for k in reads:
            st = self.res.setdefault(k, [None, {}, []])
            if dma:
                st[2].append(op.idx)
            else:
                st[1][stream] = op.idx
        for k in writes:
            self.res[k] = [op.idx, {}, []]
        deps.discard(op.idx)
        op.deps = deps
        self.ops.append(op)
        return op

    def pe(self, fn, reads=(), writes=()):
        return self._add("pe", fn, reads, writes, False)

    def act(self, fn, reads=(), writes=()):
        return self._add("act", fn, reads, writes, False)

    def dve(self, fn, reads=(), writes=()):
        return self._add("dve", fn, reads, writes, False)

    def pool(self, fn, reads=(), writes=()):
        return self._add("pool", fn, reads, writes, False)

    def dma(self, out, in_, reads=(), writes=(), q="sp"):
        return self._add(q, lambda e: e.dma_start(out=out, in_=in_), reads, writes, True)

    def emit(self):
        prog = self.prog
        nc = prog.nc
        ops = self.ops

        def skip(dop, s):
            return (not dop.dma) and dop.stream == s and (s == "pe" or not SAME_ENGINE_SYNC)

        for op in ops:
            for d in op.deps:
                dop = ops[d]
                if dop.dma or skip(dop, op.stream):
                    continue
                dop.signal = True
        for op in ops:
            if op.dma:
                n = prog.dcount[op.stream]
                prog.dcount[op.stream] = n + 1
                op.token = (prog.dsem[op.stream][n % NDMA_SEM], 16 * (n // NDMA_SEM + 1), n)
            elif op.signal:
                prog.ccount[op.stream] += 1
                op.token = (prog.csem[op.stream], prog.ccount[op.stream], None)
        per = {s: [] for s in STREAMS}
        for op in ops:
            per[op.stream].append(op)
        last_dma = []
        for s in ("sp", "act", "pool"):
            n = prog.dcount[s]
            for j in range(max(0, n - NDMA_SEM), n):
                last_dma.append((prog.dsem[s][j % NDMA_SEM], 16 * (j // NDMA_SEM + 1)))

        def run_stream(s, eng):
            known = {}

            def wait(sem, val):
                if known.get(sem.num, -1) >= val:
                    return
                eng.wait_ge(sem, val)
                known[sem.num] = val
                prog.n_instr += 1

            for op in per[s]:
                for d in sorted(op.deps):
                    dop = ops[d]
                    if skip(dop, s):
                        continue
                    wait(dop.token[0], dop.token[1])
                if op.dma:
                    sem, val, n = op.token
                    if n >= NDMA_SEM:
                        wait(sem, val - 16)
                ins = op.fn(eng)
                prog.n_instr += 1
                if op.token is not None:
                    ins.then_inc(op.token[0], 16 if op.dma else 1)
            if s == "sp":
                for sem, val in last_dma:
                    wait(sem, val)

        with nc.Block() as block:
            @block.tensor
            def _(e):
                run_stream("pe", e)

            @block.scalar
            def _(e):
                run_stream("act", e)

            @block.vector
            def _(e):
                run_stream("dve", e)

            @block.gpsimd
            def _(e):
                run_stream("pool", e)

            @block.sync
            def _(e):
                run_stream("sp", e)


class Cfg:
    def __init__(self, S, CTX, DEPTH, stop_after=None, debug=False, mixer_override=None):
        self.debug = debug
        self.mixer_override = mixer_override or {}
        self.S = S
        self.CTX = CTX
        self.DEPTH = DEPTH
        self.NL = S // 128
        self.NCT = CTX // 128
        self.NT = self.NL + self.NCT
        self.ROWS = S // GRID_W
        self.ALPHA = float(8 ** 0.25)
        self.stop_after = stop_after


class K:
    def __init__(self, cfg):
        self.cfg = cfg
        nc = bass.Bass("TRN2", target_bir_lowering=False)
        self.nc = nc
        c = cfg
        n_even = (c.DEPTH + 1) // 2
        n_odd = c.DEPTH // 2
        self.n_even, self.n_odd = n_even, max(n_odd, 1)

        def inp(name, shape):
            return nc.dram_tensor(name, list(shape), F32, kind="ExternalInput").ap()

        self.x = inp("x", [c.S, D])
        self.ctx = inp("ctx", [c.CTX, D])
        self.cc = inp("cc", [128, 16])
        self.ident = inp("ident", [128, 128])
        self.ada_w = inp("ada_w", [c.DEPTH, D, 9 * D])
        self.ada_b = inp("ada_b", [c.DEPTH, 9 * D])
        self.ffn_w_in = [inp("ffn1_w_in", [c.DEPTH, D, 2 * FF]), inp("ffn2_w_in", [c.DEPTH, D, 2 * FF])]
        self.ffn_w_out = [inp("ffn1_w_out", [c.DEPTH, FF, D]), inp("ffn2_w_out", [c.DEPTH, FF, D])]
        self.na_w_in = inp("na_w_in", [self.n_odd, D, 3 * D])
        self.na_w_out = inp("na_w_out", [self.n_odd, D, D])
        self.natab = inp("natab", [self.n_odd, 8, 16, 64, 512])
        ne = self.n_even
        self.even_w_in = inp("even_w_in", [ne, D, 2144])
        self.gla_wg2_f = inp("gla_wg2_f", [ne, 16, 256])
        self.gla_wg2_b = inp("gla_wg2_b", [ne, 16, 256])
        self.gla_bg_f = inp("gla_bg_f", [ne, 256])
        self.gla_bg_b = inp("gla_bg_b", [ne, 256])
        self.gla_norm_g = inp("gla_norm_g", [ne, 128])
        self.mla_q_norm_g = inp("mla_q_norm_g", [ne, 384])
        self.mla_kv_norm_g = inp("mla_kv_norm_g", [ne, 128])
        self.mla_w_uq = inp("mla_w_uq", [ne, 384, 768])
        self.mla_w_ukv = inp("mla_w_ukv", [ne, 128, 1024])
        self.even_w_out = inp("even_w_out", [ne, D, D])
        self.cosT = inp("cosT", [64, c.NT * 128])
        self.sinT = inp("sinT", [64, c.NT * 128])
        self.tri = inp("tri", [3, 128, 128])
        self.out = nc.dram_tensor("out", [c.S, D], F32, kind="ExternalOutput").ap()
        self.xres = self.scratch("xres", [c.NT * 128, D], F32)
        self.modd = self.scratch("modd", [c.DEPTH * 2, 9 * D], F32)
        NTOK = c.NT * 128
        self.qT = self.scratch("qT", [8, 128, NTOK], BF16)
        self.kT = self.scratch("kT", [8, 128, NTOK], BF16)
        self.vv = self.scratch("vv", [NTOK, D], BF16)
        self.gqk = self.scratch("gqk", [NTOK, 512], F32)
        self.gv = self.scratch("gv", [NTOK, 512], BF16)
        self.gr = self.scratch("gr", [NTOK, 512], F32)
        self.glog = self.scratch("glog", [NTOK, 512], F32)
        self.gof = self.scratch("gof", [NTOK, 512], F32)
        self.ycat = self.scratch("ycat", [NTOK, 512], BF16)
        self.mq = self.scratch("mq", [4, 192, NTOK], BF16)
        self.mk = self.scratch("mk", [4, 128, NTOK], BF16)
        self.mkr = self.scratch("mkr", [64, NTOK], BF16)
        self.mvv = self.scratch("mvv", [NTOK, 512], BF16)
        self.mo = self.scratch("mo", [4, 128, NTOK], BF16)

    def sbt(self, name, shape, dt):
        self._uid = getattr(self, "_uid", 0) + 1
        return self.nc.sbuf_tensor("%s_%d" % (name, self._uid), shape, dt)

    def scratch(self, name, shape, dt):
        kind = "ExternalOutput" if self.cfg.debug else "Internal"
        return self.nc.dram_tensor(name, list(shape), dt, kind=kind).ap()

    def build(self):
        nc = self.nc
        c = self.cfg
        with ExitStack() as es:
            self.prog = Prog(nc, es)
            self.ps = [es.enter_context(nc.psum_tensor("ps%d" % i, [128, 512], F32)) for i in range(7)]
            self.pT = es.enter_context(nc.psum_tensor("pT", [128, D], BF16))
            self.phase_init()
            for l in range(c.DEPTH):
                self.phase_ffn(l, 0)
                if c.stop_after == (l, 0):
                    break
                mt = c.mixer_override.get(l, "odd" if l % 2 == 1 else "even")
                if mt == "odd":
                    self.phase_na_proj(l)
                    self.phase_na_attn(l)
                else:
                    self.phase_even_proj(l)
                    if c.stop_after == (l, 0.5):
                        break
                    self.phase_gla(l)
                    if c.stop_after == (l, 0.6):
                        break
                    self.phase_mla(l)
                    if c.stop_after == (l, 0.7):
                        break
                    self.phase_even_out(l)
                if c.stop_after == (l, 1):
                    break
                self.phase_ffn(l, 1)
            self.phase_final()
        return nc

    def phase_init(self):
        nc = self.nc
        c = self.cfg
        with ExitStack() as es:
            P = Phase(self.prog)
            xb = [es.enter_context(self.sbt("ib%d" % i, [128, D], F32)) for i in range(2)]
            for t in range(c.NT):
                src = self.x[t * 128:(t + 1) * 128, :] if t < c.NL else self.ctx[(t - c.NL) * 128:(t - c.NL + 1) * 128, :]
                b = t % 2
                P.dma(xb[b][:], src, writes=[("ib", b)])
                P.dma(self.xres[t * 128:(t + 1) * 128, :], xb[b][:], reads=[("ib", b)], writes=[("xres", t)], q="pool")
            cct = es.enter_context(self.sbt("cct", [128, 16], F32))
            sct = es.enter_context(self.sbt("sct", [128, 16], F32))
            P.dma(cct[:], self.cc, writes=["cct"])
            P.act(lambda e: e.activation(out=sct[:], in_=cct[:], func=AF.Silu), reads=["cct"], writes=["sct"])
            wt = [es.enter_context(self.sbt("awt%d" % i, [128, KC, 512], F32)) for i in range(2)]
            msb = es.enter_context(self.sbt("msb", [2, 9 * D], F32))
            bsb = es.enter_context(self.sbt("bsb", [2, 9 * D], F32))
            it = 0
            for l in range(c.DEPTH):
                P.dma(bsb[:], self.ada_b[l:l + 1, :].partition_broadcast(2), reads=[], writes=["bsb"])
                for nt in range(18):
                    b = it % 2
                    psi = it % 2
                    psb = self.ps[psi]
                    it += 1
                    P.dma(wt[b][:], self.ada_w[l].rearrange("(k p) n -> p k n", p=128)[:, :, nt * 512:(nt + 1) * 512],
                          writes=[("awt", b)])
                    for k in range(KC):
                        P.pe(lambda e, k=k, b=b, psb=psb, psi=psi: e.matmul(psb[0:2, :], lhsT=sct[:, 2 * k:2 * k + 2], rhs=wt[b][:, k, :],
                                                                 start=(k == 0), stop=(k == KC - 1)),
                             reads=["sct", ("awt", b)], writes=[("ps", psi)])
                    P.dve(lambda e, nt=nt, psb=psb: e.tensor_tensor(out=msb[:, nt * 512:(nt + 1) * 512], in0=psb[0:2, :],
                                                                  in1=bsb[:, nt * 512:(nt + 1) * 512], op=ALU.add),
                          reads=[("ps", psi), "bsb"], writes=["msb"])
                for j in (1, 4, 7):
                    P.dve(lambda e, j=j: e.tensor_scalar_add(out=msb[:, j * D:(j + 1) * D], in0=msb[:, j * D:(j + 1) * D], scalar1=1.0),
                          reads=["msb"], writes=["msb"])
                for j in (2, 8):
                    P.dve(lambda e, j=j: e.tensor_scalar_mul(out=msb[:, j * D:(j + 1) * D], in0=msb[:, j * D:(j + 1) * D], scalar1=0.5),
                          reads=["msb"], writes=["msb"])
                P.dma(self.modd[2 * l:2 * l + 2, :], msb[:], reads=["msb"], writes=[("modd", l)], q="pool")
            P.emit()

    def load_bcast(self, P, tiles, l, js, r, key):
        for tl, j in zip(tiles, js):
            P.dma(tl[:], self.modd[2 * l + r:2 * l + r + 1, j * D:(j + 1) * D].partition_broadcast(128),
                  reads=[], writes=[(key, j % 3)])

    def phase_ffn(self, l, which):
        nc = self.nc
        c = self.cfg
        w_in_d = self.ffn_w_in[which][l]
        w_out_d = self.ffn_w_out[which][l]
        js = (0, 1, 2) if which == 0 else (6, 7, 8)
        G = 2
        with ExitStack() as es:
            P = Phase(self.prog)
            sb = lambda name, shape, dt: es.enter_context(self.sbt(name, shape, dt))
            win = sb("win", [128, KC, 2 * FF], BF16)
            wout = sb("wout", [128, FC, D], BF16)
            stg = [sb("stg%d" % i, [128, 1408], F32) for i in range(2)]
            identf = sb("identf", [128, 128], F32)
            identb = sb("identb", [128, 128], BF16)
            bc = [sb("bc%d" % i, [128, D], F32) for i in range(3)]
            xin = [sb("xin%d" % i, [128, D], F32) for i in range(4)]
            xmb = sb("xmb", [128, D], BF16)
            xT = sb("xT", [128, KC, G * 128], BF16)
            aT = sb("aT", [128, FC, G * 128], BF16)
            sg = [sb("sg%d" % i, [128, G * 128], F32) for i in range(2)]
            z = sb("z", [128, D], F32)
            st = sb("st", [128, 2, 6], F32)
            mv = sb("mv", [128, 2], F32)
            rs = sb("rs", [128, 1], F32)
            pT = self.pT
            P.dma(identf[:], self.ident, writes=["identf"])
            P.dve(lambda e: e.tensor_copy(out=identb[:], in_=identf[:]), reads=["identf"], writes=["identb"])
            ci = 0
            for k in range(KC):
                for q4 in range(4):
                    b = ci % 2
                    eng = P.pool if ci % 2 == 0 else P.act
                    ci += 1
                    P.dma(stg[b][:], w_in_d[k * 128:(k + 1) * 128, q4 * 1408:(q4 + 1) * 1408], writes=[("stg", b)])
                    if eng == P.pool:
                        P.pool(lambda e, b=b, k=k, q4=q4: e.tensor_copy(out=win[:, k, q4 * 1408:(q4 + 1) * 1408], in_=stg[b][:]),
                               reads=[("stg", b)], writes=["win"])
                    else:
                        P.act(lambda e, b=b, k=k, q4=q4: e.copy(out=win[:, k, q4 * 1408:(q4 + 1) * 1408], in_=stg[b][:]),
                              reads=[("stg", b)], writes=["win"])
            for f in range(FC):
                b = ci % 2
                use_pool = ci % 2 == 0
                ci += 1
                P.dma(stg[b][:, 0:D], w_out_d[f * 128:(f + 1) * 128, :], writes=[("stg", b)])
                if use_pool:
                    P.pool(lambda e, b=b, f=f: e.tensor_copy(out=wout[:, f, :], in_=stg[b][:, 0:D]), reads=[("stg", b)], writes=["wout"])
                else:
                    P.act(lambda e, b=b, f=f: e.copy(out=wout[:, f, :], in_=stg[b][:, 0:D]), reads=[("stg", b)], writes=["wout"])
            psg = [self.ps[0], self.ps[1]]
            psu = [self.ps[2], self.ps[3]]
            psy = [self.ps[4], self.ps[5]]
            gi = 0
            cur_r = None
            groups = []
            t = 0
            while t < c.NT:
                lim = c.NL if t < c.NL else c.NT
                g = min(G, lim - t)
                groups.append((t, g))
                t += g
            for (t0, g) in groups:
                r = 0 if t0 < c.NL else 1
                if r != cur_r:
                    self.load_bcast(P, bc, l, js, r, "bc")
                    cur_r = r
                n = g * 128
                for i in range(g):
                    t = t0 + i
                    xb = xin[t % 4]
                    xk = ("xin", t % 4)
                    P.dma(xb[:], self.xres[t * 128:(t + 1) * 128, :], reads=[("xres", t)], writes=[xk])
                    P.dve(lambda e, xb=xb: e.tensor_tensor(out=z[:], in0=xb[:], in1=bc[1][:], op=ALU.mult),
                          reads=[xk, ("bc", 1)], writes=["z"])
                    P.dve(lambda e: e.tensor_tensor(out=xmb[:], in0=z[:], in1=bc[0][:], op=ALU.add),
                          reads=["z", ("bc", 0)], writes=["xmb"])
                    for j in range(KC):
                        P.pe(lambda e, j=j: e.transpose(out=pT[:, j * 128:(j + 1) * 128], in_=xmb[:, j * 128:(j + 1) * 128], identity=identb[:]),
                             reads=["xmb", "identb"], writes=["pT"])
                    P.act(lambda e, i=i: e.copy(out=xT[:, :, i * 128:(i + 1) * 128], in_=pT[:].rearrange("p (k n) -> p k n", k=KC)),
                          reads=["pT"], writes=["xT"])
                for fc in range(FC):
                    b = gi % 2
                    gi += 1
                    for k in range(KC):
                        P.pe(lambda e, n=n, k=k, fc=fc, b=b: e.matmul(psg[b][:, 0:n], lhsT=win[:, k, fc * 128:(fc + 1) * 128], rhs=xT[:, k, 0:n],
                                                                start=(k == 0), stop=(k == KC - 1)),
                             reads=["win", "xT"], writes=[("psg", b)])
                    for k in range(KC):
                        P.pe(lambda e, n=n, k=k, fc=fc, b=b: e.matmul(psu[b][:, 0:n], lhsT=win[:, k, FF + fc * 128:FF + (fc + 1) * 128], rhs=xT[:, k, 0:n],
                                                                start=(k == 0), stop=(k == KC - 1)),
                             reads=["win", "xT"], writes=[("psu", b)])
                    P.act(lambda e, n=n, b=b: e.activation(out=sg[b][:, 0:n], in_=psg[b][:, 0:n], func=AF.Silu),
                          reads=[("psg", b)], writes=[("sg", b)])
                    P.dve(lambda e, n=n, b=b, fc=fc: e.tensor_tensor(out=aT[:, fc, 0:n], in0=sg[b][:, 0:n], in1=psu[b][:, 0:n], op=ALU.mult),
                          reads=[("sg", b), ("psu", b)], writes=["aT"])
                for i in range(g):
                    t = t0 + i
                    xb = xin[t % 4]
                    xk = ("xin", t % 4)
                    for nh in range(2):
                        for fc in range(FC):
                            P.pe(lambda e, nh=nh, fc=fc, i=i: e.matmul(psy[nh][:, :], lhsT=aT[:, fc, i * 128:(i + 1) * 128],
                                                                     rhs=wout[:, fc, nh * 512:(nh + 1) * 512],
                                                                     start=(fc == 0), stop=(fc == FC - 1)),
                                 reads=["aT", "wout"], writes=[("psy", nh)])
                    for nh in range(2):
                        P.dve(lambda e, nh=nh: e.tensor_tensor(out=z[:, nh * 512:(nh + 1) * 512], in0=psy[nh][:, :],
                                                             in1=bc[2][:, nh * 512:(nh + 1) * 512], op=ALU.mult),
                              reads=[("psy", nh), ("bc", 2)], writes=["z"])
                    self.post_norm(P, z, xb, xk, st, mv, rs, t)
            P.emit()


    def groups(self, G):
        c = self.cfg
        out = []
        t = 0
        while t < c.NT:
            lim = c.NL if t < c.NL else c.NT
            g = min(G, lim - t)
            out.append((t, g))
            t += g
        return out

    def mod_transpose(self, P, t, xb, xk, bc, z, xmb, identb, hT, col):
        pT = self.pT
        P.dma(xb[:], self.xres[t * 128:(t + 1) * 128, :], reads=[("xres", t)], writes=[xk])
        P.dve(lambda e: e.tensor_tensor(out=z[:], in0=xb[:], in1=bc[1][:], op=ALU.mult), reads=[xk, ("bc", 1)], writes=["z"])
        P.dve(lambda e: e.tensor_tensor(out=xmb[:], in0=z[:], in1=bc[0][:], op=ALU.add), reads=["z", ("bc", 0)], writes=["xmb"])
        for j in range(KC):
            P.pe(lambda e, j=j: e.transpose(out=pT[:, j * 128:(j + 1) * 128], in_=xmb[:, j * 128:(j + 1) * 128], identity=identb[:]),
                 reads=["xmb", "identb"], writes=["pT"])
        P.act(lambda e: e.copy(out=hT[:, :, col:col + 128], in_=pT[:].rearrange("p (k n) -> p k n", k=KC)),
              reads=["pT"], writes=["hT"])

    def load_ident(self, P, identf, identb):
        P.dma(identf[:], self.ident, writes=["identf"])
        P.dve(lambda e: e.tensor_copy(out=identb[:], in_=identf[:]), reads=["identf"], writes=["identb"])

    def load_cast(self, P, stg, dst_fn, src_fn, n, key, width):
        for i in range(n):
            b = self._ci % 2
            use_pool = (self._ci % 2 == 0)
            self._ci += 1
            P.dma(stg[b][:, 0:width], src_fn(i), writes=[("stg", b)])
            if use_pool:
                P.pool(lambda e, b=b, i=i: e.tensor_copy(out=dst_fn(i), in_=stg[b][:, 0:width]), reads=[("stg", b)], writes=[key])
            else:
                P.act(lambda e, b=b, i=i: e.copy(out=dst_fn(i), in_=stg[b][:, 0:width]), reads=[("stg", b)], writes=[key])

    def phase_na_proj(self, l):
        nc = self.nc
        c = self.cfg
        i_odd = l // 2
        w_in_d = self.na_w_in[i_odd]
        G = 4
        self._ci = 0
        with ExitStack() as es:
            P = Phase(self.prog)
            sb = lambda name, shape, dt: es.enter_context(self.sbt(name, shape, dt))
            win = sb("win", [128, KC, 3 * D], BF16)
            stg = [sb("stg%d" % i, [128, 1536], F32) for i in range(2)]
            identf = sb("identf", [128, 128], F32)
            identb = sb("identb", [128, 128], BF16)
            bc = [sb("bc%d" % i, [128, D], F32) for i in range(2)]
            xin = [sb("xin%d" % i, [128, D], F32) for i in range(2)]
            z = sb("z", [128, D], F32)
            xmb = sb("xmb", [128, D], BF16)
            hT = sb("hT", [128, KC, G * 128], BF16)
            qsb = [sb("qsb%d" % i, [128, G * 128], BF16) for i in range(2)]
            vsb = [sb("vsb%d" % i, [128, D], BF16) for i in range(2)]
            self.load_ident(P, identf, identb)
            self.load_cast(P, stg, lambda i: win[:, i // 2, (i % 2) * 1536:(i % 2 + 1) * 1536],
                           lambda i: w_in_d[(i // 2) * 128:(i // 2 + 1) * 128, (i % 2) * 1536:(i % 2 + 1) * 1536], 16, "win", 1536)
            cur_r = None
            ei = 0
            vi = 0
            for (t0, g) in self.groups(G):
                r = 0 if t0 < c.NL else 1
                if r != cur_r:
                    self.load_bcast(P, bc, l, (3, 4), r, "bc")
                    cur_r = r
                n = g * 128
                for i in range(g):
                    t = t0 + i
                    self.mod_transpose(P, t, xin[t % 2], ("xin", t % 2), bc, z, xmb, identb, hT, i * 128)
                for which, dst in ((0, self.qT), (1, self.kT)):
                    for p in range(8):
                        b = ei % 2
                        ei += 1
                        psb = self.ps[b]
                        for k in range(KC):
                            P.pe(lambda e, n=n, k=k, p=p, which=which, psb=psb: e.matmul(psb[:, 0:n], lhsT=win[:, k, which * D + p * 128:which * D + (p + 1) * 128],
                                                                                 rhs=hT[:, k, 0:n], start=(k == 0), stop=(k == KC - 1)),
                                 reads=["win", "hT"], writes=[("psq", b)])
                        sc = 0.125 if which == 0 else 1.0
                        P.act(lambda e, n=n, b=b, psb=psb, sc=sc: e.mul(out=qsb[b][:, 0:n], in_=psb[:, 0:n], mul=sc), reads=[("psq", b)], writes=[("qsb", b)])
                        P.dma(dst[p, :, t0 * 128:t0 * 128 + n], qsb[b][:, 0:n], reads=[("qsb", b)], writes=[("qk", which, p, t0)], q="pool")
                for i in range(g):
                    t = t0 + i
                    vb = vi % 2
                    vi += 1
                    for nh in range(2):
                        psb = self.ps[2 + nh]
                        for k in range(KC):
                            P.pe(lambda e, k=k, nh=nh, i=i, psb=psb: e.matmul(psb[:, :], lhsT=hT[:, k, i * 128:(i + 1) * 128],
                                                                           rhs=win[:, k, 2 * D + nh * 512:2 * D + (nh + 1) * 512],
                                                                           start=(k == 0), stop=(k == KC - 1)),
                                 reads=["win", "hT"], writes=[("psv", nh)])
                        P.dve(lambda e, nh=nh, vb=vb, psb=psb: e.tensor_copy(out=vsb[vb][:, nh * 512:(nh + 1) * 512], in_=psb[:, :]),
                              reads=[("psv", nh)], writes=[("vsb", vb)])
                    P.dma(self.vv[t * 128:(t + 1) * 128, :], vsb[vb][:], reads=[("vsb", vb)], writes=[("vv", t)], q="pool")
            P.emit()

    def phase_na_attn(self, l):
        nc = self.nc
        c = self.cfg
        i_odd = l // 2
        S_, CTX = c.S, c.CTX
        self._ci = 0
        with ExitStack() as es:
            P = Phase(self.prog)
            sb = lambda name, shape, dt: es.enter_context(self.sbt(name, shape, dt))
            biasA = sb("biasA", [64, 16, 512], F32)
            biasB = sb("biasB", [64, 16, 512], F32)
            wout = sb("wout", [64, 16, D], BF16)
            stg = [sb("stg%d" % i, [64, D], F32) for i in range(2)]
            identf = sb("identf", [128, 128], F32)
            identb = sb("identb", [128, 128], BF16)
            kctx = sb("kctx", [128, 8, CTX], BF16)
            vctx = sb("vctx", [128, c.NCT, D], BF16)
            qrow = [sb("qrow%d" % i, [128, 8, 64], BF16) for i in range(2)]
            kwin = [sb("kwin%d" % i, [128, 8, 512], BF16) for i in range(2)]
            vwin = [sb("vwin%d" % i, [128, 4, D], BF16) for i in range(2)]
            Ssb = [sb("Ssb%d" % i, [64, 768], F32) for i in range(2)]
            Esb = [sb("Esb%d" % i, [64, 768], F32) for i in range(2)]
            Pbf = [sb("Pbf%d" % i, [64, 768], BF16) for i in range(2)]
            PT = [sb("PT%d" % i, [128, 384], BF16) for i in range(2)]
            small = [sb("small%d" % i, [64, 4], F32) for i in range(2)]
            OT = sb("OT", [64, 16, 128], BF16)
            bc = [sb("bc%d" % i, [128, D], F32) for i in range(1)]
            xin = [sb("xin%d" % i, [128, D], F32) for i in range(2)]
            z = sb("z", [128, D], F32)
            st = sb("st", [128, 2, 6], F32)
            mv = sb("mv", [128, 2], F32)
            rs = sb("rs", [128, 1], F32)
            pT = self.pT
            self.load_ident(P, identf, identb)
            P.dma(biasA[:], self.natab[i_odd, 4].rearrange("h q k -> q h k"), writes=["biasA"])
            w_out_d = self.na_w_out[i_odd]
            self.load_cast(P, stg, lambda i: wout[:, i, :], lambda i: w_out_d[i * 64:(i + 1) * 64, :], 16, "wout", D)
            P.dma(kctx[:], self.kT.rearrange("a p n -> p a n")[:, :, S_:S_ + CTX], writes=["kctx"])
            P.dma(vctx[:], self.vv[S_:S_ + CTX, :].rearrange("(c p) d -> p c d", p=128), writes=["vctx"])
            cur_r = None
            ri = 0
            hi = 0
            for t in range(c.NT):
                lat = t < c.NL
                r_ = 0 if lat else 1
                if r_ != cur_r:
                    P.dma(bc[0][:], self.modd[2 * l + r_:2 * l + r_ + 1, 5 * D:6 * D].partition_broadcast(128), writes=[("bc", 0)])
                    cur_r = r_
                xb = xin[t % 2]
                xk = ("xin", t % 2)
                P.dma(xb[:], self.xres[t * 128:(t + 1) * 128, :], reads=[("xres", t)], writes=[xk])
                for rr in range(2):
                    rb = ri % 2
                    ri += 1
                    tok0 = t * 128 + rr * 64
                    P.dma(qrow[rb][:], self.qT.rearrange("a p n -> p a n")[:, :, tok0:tok0 + 64], writes=[("qrow", rb)])
                    if lat:
                        r = 2 * t + rr
                        rs_ = min(max(r - 4, 0), c.ROWS - 8)
                        cls = r - rs_
                        P.dma(kwin[rb][:], self.kT.rearrange("a p n -> p a n")[:, :, rs_ * 64:rs_ * 64 + 512], writes=[("kwin", rb)])
                        P.dma(vwin[rb][:], self.vv[rs_ * 64:rs_ * 64 + 512, :].rearrange("(c p) d -> p c d", p=128), writes=[("vwin", rb)])
                        if cls == 4:
                            bias, bkey = biasA, "biasA"
                        else:
                            P.dma(biasB[:], self.natab[i_odd, cls].rearrange("h q k -> q h k"), writes=["biasB"])
                            bias, bkey = biasB, "biasB"
                        nk = 768
                        nch = 6
                    else:
                        nk = CTX
                        nch = CTX // 128
                    for h in range(16):
                        b = hi % 2
                        hi += 1
                        p = h // 2
                        hp = (h % 2) * 64
                        psS = self.ps[b]
                        psC = self.ps[2]
                        psO = self.ps[3]
                        S = Ssb[b]
                        E = Esb[b]
                        Pb = Pbf[b]
                        sm = small[b]
                        c0 = 512 if lat else 0
                        if lat:
                            P.pe(lambda e, rb=rb, p=p, hp=hp, psS=psS: e.matmul(psS[0:64, :], lhsT=qrow[rb][hp:hp + 64, p, :], rhs=kwin[rb][hp:hp + 64, p, :],
                                                                             start=True, stop=True),
                                 reads=[("qrow", rb), ("kwin", rb)], writes=[("psS", b)])
                            P.dve(lambda e, psS=psS, S=S, bias=bias, h=h: e.tensor_tensor(out=S[:, 0:512], in0=psS[0:64, :], in1=bias[:, h, :], op=ALU.add),
                                  reads=[("psS", b), bkey], writes=[("S", b)])
                        P.pe(lambda e, rb=rb, p=p, hp=hp, b=b: e.matmul(psC[0:64, b * 256:b * 256 + CTX], lhsT=qrow[rb][hp:hp + 64, p, :], rhs=kctx[hp:hp + 64, p, :],
                                                                     start=True, stop=True),
                             reads=[("qrow", rb), "kctx"], writes=[("psC", b)])
                        P.act(lambda e, S=S, b=b, c0=c0: e.copy(out=S[:, c0:c0 + CTX], in_=psC[0:64, b * 256:b * 256 + CTX]),
                              reads=[("psC", b)], writes=[("S", b)])
                        P.dve(lambda e, S=S, sm=sm, nk=nk: e.reduce_max(out=sm[:, 0:1], in_=S[:, 0:nk], axis=AX.X), reads=[("S", b)], writes=[("sm", b)])
                        P.dve(lambda e, sm=sm: e.tensor_scalar_mul(out=sm[:, 1:2], in0=sm[:, 0:1], scalar1=-1.0), reads=[("sm", b)], writes=[("sm", b)])
                        P.act(lambda e, S=S, E=E, sm=sm, nk=nk: e.activation(out=E[:, 0:nk], in_=S[:, 0:nk], func=AF.Exp, bias=sm[:, 1:2], scale=1.0,
                                                                          accum_out=sm[:, 2:3]),
                              reads=[("S", b), ("sm", b)], writes=[("E", b), ("sm", b)])
                        P.dve(lambda e, sm=sm: e.reciprocal(out=sm[:, 3:4], in_=sm[:, 2:3]), reads=[("sm", b)], writes=[("sm", b)])
                        P.dve(lambda e, E=E, Pb=Pb, sm=sm, nk=nk: e.tensor_scalar_mul(out=Pb[:, 0:nk], in0=E[:, 0:nk], scalar1=sm[:, 3:4]),
                              reads=[("E", b), ("sm", b)], writes=[("Pb", b)])
                        for cch in range(nch):
                            P.pe(lambda e, Pb=Pb, cch=cch, b=b: e.transpose(out=pT[:, b * 512 + cch * 64:b * 512 + (cch + 1) * 64],
                                                                          in_=Pb[:, cch * 128:(cch + 1) * 128], identity=identb[0:64, 0:64]),
                                 reads=[("Pb", b), "identb"], writes=[("pTn", b)])
                        P.act(lambda e, b=b, nch=nch: e.copy(out=PT[b][:, 0:nch * 64], in_=pT[:, b * 512:b * 512 + nch * 64]),
                              reads=[("pTn", b)], writes=[("PT", b)])
                        oc = (hi % 8) * 64
                        for cch in range(nch):
                            if lat and cch < 4:
                                lh = vwin[rb][:, cch, h * 64:(h + 1) * 64]
                                rk = ("vwin", rb)
                            else:
                                cc_ = cch - 4 if lat else cch
                                lh = vctx[:, cc_, h * 64:(h + 1) * 64]
                                rk = "vctx"
                            P.pe(lambda e, lh=lh, b=b, cch=cch, oc=oc, nch=nch: e.matmul(psO[0:64, oc:oc + 64], lhsT=lh, rhs=PT[b][:, cch * 64:(cch + 1) * 64],
                                                                                      start=(cch == 0), stop=(cch == nch - 1)),
                                 reads=[rk, ("PT", b)], writes=[("psO", oc)])
                        P.act(lambda e, h=h, rr=rr, oc=oc: e.copy(out=OT[:, h, rr * 64:(rr + 1) * 64], in_=psO[0:64, oc:oc + 64]),
                              reads=[("psO", oc)], writes=["OT"])
                for nh in range(2):
                    psy = self.ps[4 + nh]
                    for h in range(16):
                        P.pe(lambda e, h=h, nh=nh, psy=psy: e.matmul(psy[:, :], lhsT=OT[:, h, :], rhs=wout[:, h, nh * 512:(nh + 1) * 512],
                                                                   start=(h == 0), stop=(h == 15)),
                             reads=["OT", "wout"], writes=[("psy", nh)])
                    P.dve(lambda e, nh=nh, psy=psy: e.tensor_tensor(out=z[:, nh * 512:(nh + 1) * 512], in0=psy[:, :],
                                                                  in1=bc[0][:, nh * 512:(nh + 1) * 512], op=ALU.mult),
                          reads=[("psy", nh), ("bc", 0)], writes=["z"])
                self.post_norm(P, z, xb, xk, st, mv, rs, t)
            P.emit()


    def nps(self):
        self._psi = (getattr(self, "_psi", -1) + 1) % 7
        return self.ps[self._psi], ("psr", self._psi)

    def phase_even_proj(self, l):
        nc = self.nc
        c = self.cfg
        ie = l // 2
        G = 4
        self._ci = 0
        MS = float(192 ** -0.5)
        w_in_d = self.even_w_in[ie]
        with ExitStack() as es:
            P = Phase(self.prog)
            sb = lambda name, shape, dt: es.enter_context(self.sbt(name, shape, dt))
            win = sb("win", [128, KC, 2144], BF16)
            stg = [sb("stg%d" % i, [128, 1072], F32) for i in range(2)]
            wsw = sb("wsw", [128, KC, 64], BF16)
            wuq = sb("wuq", [128, 3, 768], BF16)
            wuqsw = sb("wuqsw", [128, 3, 256], BF16)
            wukv = sb("wukv", [128, 1024], BF16)
            wv = sb("wv", [128, 512], BF16)
            W2f = sb("W2f", [32, 512], F32)
            W2 = sb("W2", [32, 512], BF16)
            bgb = sb("bgb", [128, 512], F32)
            gnb = sb("gnb", [128, 512], F32)
            identf = sb("identf", [128, 128], F32)
            identb = sb("identb", [128, 128], BF16)
            bc = [sb("bc%d" % i, [128, D], F32) for i in range(2)]
            xin = [sb("xin%d" % i, [128, D], F32) for i in range(2)]
            z = sb("z", [128, D], F32)
            xmb = sb("xmb", [128, D], BF16)
            hT = sb("hT", [128, KC, G * 128], BF16)
            cT = sb("cT", [128, 4, G * 128], BF16)
            lrT = sb("lrT", [32, G * 128], BF16)
            tf = [sb("tf%d" % i, [128, 512], F32) for i in range(2)]
            tb = [sb("tb%d" % i, [128, 512], BF16) for i in range(2)]
            cs = [sb("cs%d" % i, [64, G * 128], F32) for i in range(2)]
            r1 = sb("r1", [64, G * 128], F32)
            r2 = sb("r2", [64, G * 128], F32)
            junk = sb("junk", [128, 512], F32)
            sq = sb("sq", [128, 4], F32)
            self.load_ident(P, identf, identb)
            self.load_cast(P, stg, lambda i: win[:, i // 2, (i % 2) * 1072:(i % 2 + 1) * 1072],
                           lambda i: w_in_d[(i // 2) * 128:(i // 2 + 1) * 128, (i % 2) * 1072:(i % 2 + 1) * 1072], 16, "win", 1072)
            for k in range(KC):
                src = win[:, k, 2080:2144].rearrange("p (i two) -> p i two", two=2)
                dst = wsw[:, k, :].rearrange("p (i two) -> p i two", two=2)
                P.dve(lambda e, src=src, dst=dst: e.tensor_scalar_mul(out=dst[:, :, 0], in0=src[:, :, 1], scalar1=-1.0), reads=["win"], writes=["wsw"])
                P.dve(lambda e, src=src, dst=dst: e.tensor_copy(out=dst[:, :, 1], in_=src[:, :, 0]), reads=["win"], writes=["wsw"])
            uq_d = self.mla_w_uq[ie]
            self.load_cast(P, stg, lambda i: wuq[:, i, :], lambda i: uq_d[i * 128:(i + 1) * 128, :], 3, "wuq", 768)
            for j in range(3):
                for h in range(4):
                    src = wuq[:, j, h * 192 + 128:h * 192 + 192].rearrange("p (i two) -> p i two", two=2)
                    dst = wuqsw[:, j, h * 64:(h + 1) * 64].rearrange("p (i two) -> p i two", two=2)
                    P.dve(lambda e, src=src, dst=dst: e.tensor_scalar_mul(out=dst[:, :, 0], in0=src[:, :, 1], scalar1=-1.0), reads=["wuq"], writes=["wuqsw"])
                    P.dve(lambda e, src=src, dst=dst: e.tensor_copy(out=dst[:, :, 1], in_=src[:, :, 0]), reads=["wuq"], writes=["wuqsw"])
            ukv_d = self.mla_w_ukv[ie]
            self.load_cast(P, stg, lambda i: wukv[:, :], lambda i: ukv_d[:, :], 1, "wukv", 1024)
            for h in range(4):
                P.dve(lambda e, h=h: e.tensor_copy(out=wv[:, h * 128:(h + 1) * 128], in_=wukv[:, h * 256 + 128:h * 256 + 256]), reads=["wukv"], writes=["wv"])
            P.dve(lambda e: e.memset(W2f[:], 0.0), writes=["W2f"])
            P.dma(W2f[0:16, 0:256], self.gla_wg2_f[ie], reads=["W2f"], writes=["W2f"])
            P.dma(W2f[16:32, 256:512], self.gla_wg2_b[ie], reads=["W2f"], writes=["W2f"])
            P.dve(lambda e: e.tensor_copy(out=W2[:], in_=W2f[:]), reads=["W2f"], writes=["W2"])
            P.dma(bgb[:, 0:256], self.gla_bg_f[ie:ie + 1, :].partition_broadcast(128), writes=["bgb"])
            P.dma(bgb[:, 256:512], self.gla_bg_b[ie:ie + 1, :].partition_broadcast(128), reads=["bgb"], writes=["bgb"])
            P.dma(gnb[:, 0:384], self.mla_q_norm_g[ie:ie + 1, :].partition_broadcast(128), writes=["gnb"])
            P.dma(gnb[:, 384:512], self.mla_kv_norm_g[ie:ie + 1, :].partition_broadcast(128), reads=["gnb"], writes=["gnb"])
            cur_r = None
            ti = 0
            for (t0, g) in self.groups(G):
                r = 0 if t0 < c.NL else 1
                if r != cur_r:
                    self.load_bcast(P, bc, l, (3, 4), r, "bc")
                    cur_r = r
                n = g * 128
                c0 = t0 * 128
                for i in range(g):
                    t = t0 + i
                    self.mod_transpose(P, t, xin[t % 2], ("xin", t % 2), bc, z, xmb, identb, hT, i * 128)
                P.dma(cs[0][:, 0:n], self.cosT[:, c0:c0 + n], writes=[("cs", 0)])
                P.dma(cs[1][:, 0:n], self.sinT[:, c0:c0 + n], writes=[("cs", 1)])

                def tok_mm(i, col0, ncols):
                    psb, pk = self.nps()
                    for k in range(KC):
                        P.pe(lambda e, k=k, i=i, psb=psb: e.matmul(psb[:, 0:ncols], lhsT=hT[:, k, i * 128:(i + 1) * 128], rhs=win[:, k, col0:col0 + ncols],
                                                                 start=(k == 0), stop=(k == KC - 1)),
                             reads=["hT", "win"], writes=[pk])
                    return psb, pk

                def feat_mm(lhs_fn, nkc, rhs_fn, M):
                    psb, pk = self.nps()
                    for k in range(nkc):
                        P.pe(lambda e, k=k, psb=psb, n=n: e.matmul(psb[0:M, 0:n], lhsT=lhs_fn(k), rhs=rhs_fn(k), start=(k == 0), stop=(k == nkc - 1)),
                             reads=["hT", "win", "cT", "wuq", "wuqsw", "wukv", "wsw"], writes=[pk])
                    return psb, pk

                for i in range(g):
                    t = t0 + i
                    rows = slice(t * 128, (t + 1) * 128)
                    psb, pk = tok_mm(i, 0, 512)
                    b = ti % 2
                    ti += 1
                    P.act(lambda e, psb=psb, b=b: e.mul(out=tf[b][:, 0:256], in_=psb[:, 0:256], mul=0.125), reads=[pk], writes=[("tf", b)])
                    P.dve(lambda e, psb=psb, b=b: e.tensor_copy(out=tf[b][:, 256:512], in_=psb[:, 256:512]), reads=[pk], writes=[("tf", b)])
                    P.dma(self.gqk[rows, :], tf[b][:], reads=[("tf", b)], writes=[("gqk", t)], q="pool")
                    psb, pk = tok_mm(i, 512, 512)
                    P.dve(lambda e, psb=psb, b=b: e.tensor_copy(out=tb[b][:], in_=psb[:, :]), reads=[pk], writes=[("tb", b)])
                    P.dma(self.gv[rows, :], tb[b][:], reads=[("tb", b)], writes=[("gv", t)], q="pool")
                    psb, pk = tok_mm(i, 1024, 512)
                    b = ti % 2
                    ti += 1
                    P.act(lambda e, psb=psb, b=b: e.activation(out=tf[b][:], in_=psb[:, :], func=AF.Silu), reads=[pk], writes=[("tf", b)])
                    P.dma(self.gr[rows, :], tf[b][:], reads=[("tf", b)], writes=[("gr", t)], q="pool")
                    psb, pk = tok_mm(i, 1568, 512)
                    P.act(lambda e, psb=psb: e.activation(out=junk[:, 0:384], in_=psb[:, 0:384], func=AF.Square, accum_out=sq[:, 0:1]),
                          reads=[pk], writes=["junk", "sq"])
                    P.act(lambda e, psb=psb: e.activation(out=junk[:, 384:512], in_=psb[:, 384:512], func=AF.Square, accum_out=sq[:, 1:2]),
                          reads=[pk], writes=["junk", "sq"])
                    P.act(lambda e: e.activation(out=sq[:, 2:3], in_=sq[:, 0:1], func=AF.Sqrt, bias=EPS, scale=1.0 / 384), reads=["sq"], writes=["sq"])
                    P.act(lambda e: e.activation(out=sq[:, 3:4], in_=sq[:, 1:2], func=AF.Sqrt, bias=EPS, scale=1.0 / 128), reads=["sq"], writes=["sq"])
                    P.dve(lambda e: e.reciprocal(out=sq[:, 2:4], in_=sq[:, 2:4]), reads=["sq"], writes=["sq"])
                    b = ti % 2
                    ti += 1
                    P.dve(lambda e, psb=psb, b=b: e.scalar_tensor_tensor(out=tb[b][:, 0:384], in0=psb[:, 0:384], scalar=sq[:, 2:3], in1=gnb[:, 0:384],
                                                                       op0=ALU.mult, op1=ALU.mult), reads=[pk, "sq", "gnb"], writes=[("tb", b)])
                    P.dve(lambda e, psb=psb, b=b: e.scalar_tensor_tensor(out=tb[b][:, 384:512], in0=psb[:, 384:512], scalar=sq[:, 3:4], in1=gnb[:, 384:512],
                                                                       op0=ALU.mult, op1=ALU.mult), reads=[pk, "sq", "gnb"], writes=[("tb", b)])
                    for j in range(4):
                        P.pe(lambda e, j=j, b=b: e.transpose(out=self.pT[:, j * 128:(j + 1) * 128], in_=tb[b][:, j * 128:(j + 1) * 128], identity=identb[:]),
                             reads=[("tb", b), "identb"], writes=["pT"])
                    P.act(lambda e, i=i: e.copy(out=cT[:, :, i * 128:(i + 1) * 128], in_=self.pT[:, 0:512].rearrange("p (k n) -> p k n", k=4)),
                          reads=["pT"], writes=["cT"])
                psb, pk = feat_mm(lambda k: win[:, k, 1536:1568], KC, lambda k, n=n: hT[:, k, 0:n], 32)
                P.act(lambda e, psb=psb, n=n: e.copy(out=lrT[:, 0:n], in_=psb[0:32, 0:n]), reads=[pk], writes=["lrT"])
                for i in range(g):
                    t = t0 + i
                    rows = slice(t * 128, (t + 1) * 128)
                    psb, pk = self.nps()
                    P.pe(lambda e, i=i, psb=psb: e.matmul(psb[:, :], lhsT=lrT[:, i * 128:(i + 1) * 128], rhs=W2[:], start=True, stop=True),
                         reads=["lrT", "W2"], writes=[pk])
                    b = ti % 2
                    ti += 1
                    P.dve(lambda e, psb=psb, b=b: e.tensor_tensor(out=tf[b][:], in0=psb[:, :], in1=bgb[:], op=ALU.add), reads=[pk, "bgb"], writes=[("tf", b)])
                    P.act(lambda e, b=b: e.activation(out=tf[b][:], in_=tf[b][:], func=AF.Exp, scale=-1.0), reads=[("tf", b)], writes=[("tf", b)])
                    P.act(lambda e, b=b: e.activation(out=tf[b][:], in_=tf[b][:], func=AF.Ln, bias=1.0, scale=1.0), reads=[("tf", b)], writes=[("tf", b)])
                    P.dve(lambda e, b=b: e.tensor_scalar_mul(out=tf[b][:], in0=tf[b][:], scalar1=-1.0 / 16.0), reads=[("tf", b)], writes=[("tf", b)])
                    P.dma(self.glog[rows, :], tf[b][:], reads=[("tf", b)], writes=[("glog", t)], q="pool")

                def rope_store(psa, pka, psb2, pkb, scale, dst):
                    P.dve(lambda e, psa=psa, n=n: e.tensor_tensor(out=r1[:, 0:n], in0=psa[0:64, 0:n], in1=cs[0][:, 0:n], op=ALU.mult),
                          reads=[pka, ("cs", 0)], writes=["r1"])
                    P.dve(lambda e, psb2=psb2, n=n: e.tensor_tensor(out=r2[:, 0:n], in0=psb2[0:64, 0:n], in1=cs[1][:, 0:n], op=ALU.mult),
                          reads=[pkb, ("cs", 1)], writes=["r2"])
                    b = self._rb = (getattr(self, "_rb", 0) + 1) % 2
                    P.dve(lambda e, n=n, b=b: e.scalar_tensor_tensor(out=tb[b][0:64, 0:n], in0=r1[:, 0:n], scalar=scale, in1=r2[:, 0:n], op0=ALU.mult, op1=ALU.add),
                          reads=["r1", "r2"], writes=[("tb", b)])
                    P.dma(dst, tb[b][0:64, 0:n], reads=[("tb", b)], writes=[("ropedst", str(dst.offset), c0)], q="pool")

                psa, pka = feat_mm(lambda k: win[:, k, 2080:2144], KC, lambda k, n=n: hT[:, k, 0:n], 64)
                psb2, pkb = feat_mm(lambda k: wsw[:, k, :], KC, lambda k, n=n: hT[:, k, 0:n], 64)
                if MS != 1.0:
                    pass
                rope_store(psa, pka, psb2, pkb, 1.0, self.mkr[:, c0:c0 + n])
                for h in range(4):
                    psb, pk = feat_mm(lambda k, h=h: wuq[:, k, h * 192:h * 192 + 128], 3, lambda k, n=n: cT[:, k, 0:n], 128)
                    b = ti % 2
                    ti += 1
                    P.act(lambda e, psb=psb, b=b, n=n: e.mul(out=tb[b][:, 0:n], in_=psb[:, 0:n], mul=MS), reads=[pk], writes=[("tb", b)])
                    P.dma(self.mq[h, 0:128, c0:c0 + n], tb[b][:, 0:n], reads=[("tb", b)], writes=[("mq", h, c0)], q="pool")
                    psa, pka = feat_mm(lambda k, h=h: wuq[:, k, h * 192 + 128:h * 192 + 192], 3, lambda k, n=n: cT[:, k, 0:n], 64)
                    psb2, pkb = feat_mm(lambda k, h=h: wuqsw[:, k, h * 64:(h + 1) * 64], 3, lambda k, n=n: cT[:, k, 0:n], 64)
                    P.dve(lambda e, psb2=psb2, n=n: e.tensor_scalar_mul(out=r2[:, 0:n], in0=psb2[0:64, 0:n], scalar1=MS), reads=[pkb], writes=["r2"])
                    P.dve(lambda e, n=n: e.tensor_tensor(out=r2[:, 0:n], in0=r2[:, 0:n], in1=cs[1][:, 0:n], op=ALU.mult), reads=["r2", ("cs", 1)], writes=["r2"])
                    P.dve(lambda e, psa=psa, n=n: e.tensor_tensor(out=r1[:, 0:n], in0=psa[0:64, 0:n], in1=cs[0][:, 0:n], op=ALU.mult),
                          reads=[pka, ("cs", 0)], writes=["r1"])
                    b = ti % 2
                    ti += 1
                    P.dve(lambda e, n=n, b=b: e.scalar_tensor_tensor(out=tb[b][0:64, 0:n], in0=r1[:, 0:n], scalar=MS, in1=r2[:, 0:n], op0=ALU.mult, op1=ALU.add),
                          reads=["r1", "r2"], writes=[("tb", b)])
                    P.dma(self.mq[h, 128:192, c0:c0 + n], tb[b][0:64, 0:n], reads=[("tb", b)], writes=[("mqr", h, c0)], q="pool")
                    psb, pk = feat_mm(lambda k, h=h: wukv[:, h * 256:h * 256 + 128], 1, lambda k, n=n: cT[:, 3, 0:n], 128)
                    b = ti % 2
                    ti += 1
                    P.act(lambda e, psb=psb, b=b, n=n: e.copy(out=tb[b][:, 0:n], in_=psb[:, 0:n]), reads=[pk], writes=[("tb", b)])
                    P.dma(self.mk[h, :, c0:c0 + n], tb[b][:, 0:n], reads=[("tb", b)], writes=[("mk", h, c0)], q="pool")
                for i in range(g):
                    t = t0 + i
                    psb, pk = self.nps()
                    P.pe(lambda e, i=i, psb=psb: e.matmul(psb[:, :], lhsT=cT[:, 3, i * 128:(i + 1) * 128], rhs=wv[:], start=True, stop=True),
                         reads=["cT", "wv"], writes=[pk])
                    b = ti % 2
                    ti += 1
                    P.dve(lambda e, psb=psb, b=b: e.tensor_copy(out=tb[b][:], in_=psb[:, :]), reads=[pk], writes=[("tb", b)])
                    P.dma(self.mvv[t * 128:(t + 1) * 128, :], tb[b][:], reads=[("tb", b)], writes=[("mvv", t)], q="pool")
            P.emit()


    def phase_gla(self, l):
        nc = self.nc
        c = self.cfg
        ie = l // 2
        with ExitStack() as es:
            P = Phase(self.prog)
            sb = lambda name, shape, dt: es.enter_context(self.sbt(name, shape, dt))
            trif = sb("trif", [128, 3, 128], F32)
            onesf = sb("onesf", [128, 128], F32)
            trib = sb("trib", [128, 3, 128], BF16)
            onesb = sb("onesb", [128, 128], BF16)
            ghl = [sb("ghl%d" % i, [128, 2, 256], BF16) for i in range(2)]
            mask4 = [sb("mask4_%d" % i, [128, 4, 128], F32) for i in range(2)]
            identf = sb("identf", [128, 128], F32)
            identb = sb("identb", [128, 128], BF16)
            gnb = sb("gnb", [128, 128], F32)
            qk = [sb("qk%d" % i, [128, 512], F32) for i in range(2)]
            vb = [sb("vb%d" % i, [128, 512], BF16) for i in range(2)]
            gl = [sb("gl%d" % i, [128, 256], F32) for i in range(2)]
            eb = sb("eb", [128, 256], F32)
            enb = sb("enb", [128, 256], F32)
            ebl = sb("ebl", [128, 256], F32)
            Qt = sb("Qt", [128, 256], BF16)
            Kt = sb("Kt", [128, 256], BF16)
            Kd = sb("Kd", [128, 256], BF16)
            dec = sb("dec", [128, 32], F32)
            QKT = [sb("QKT%d" % i, [128, 4, 128], BF16) for i in range(2)]
            attT = sb("attT", [128, 512], BF16)
            Qz = [sb("Qz%d" % i, [128, 4, 128], BF16) for i in range(2)]
            Sst = sb("Sst", [128, 2, 128], F32)
            Sbf = sb("Sbf", [128, 4, 128], BF16)
            osb = [sb("osb%d" % i, [128, 512], F32) for i in range(2)]
            ofl = [sb("ofl%d" % i, [128, 512], F32) for i in range(2)]
            grl = [sb("grl%d" % i, [128, 512], F32) for i in range(2)]
            yb = [sb("yb%d" % i, [128, 512], BF16) for i in range(2)]
            junk = sb("junk", [128, 128], F32)
            sq = sb("sq", [128, 8], F32)
            self.load_ident(P, identf, identb)
            P.dma(trif[:], self.tri.rearrange("a p n -> p a n"), writes=["trif"])
            P.dve(lambda e: e.memset(onesf[:], 1.0), writes=["onesf"])
            P.dve(lambda e: e.memset(onesb[:], 1.0), writes=["onesb"])
            P.dve(lambda e: e.tensor_copy(out=trib[:], in_=trif[:]), reads=["trif"], writes=["trib"])
            for h in range(4):
                P.dve(lambda e, h=h: e.tensor_copy(out=mask4[0][:, h, :], in_=trif[:, 0, :]), reads=["trif"], writes=["mask4"])
                P.dve(lambda e, h=h: e.tensor_copy(out=mask4[1][:, h, :], in_=trif[:, 2, :]), reads=["trif"], writes=["mask4"])
            P.dma(gnb[:], self.gla_norm_g[ie:ie + 1, :].partition_broadcast(128), writes=["gnb"])
            for i in range(2):
                P.dve(lambda e, i=i: e.memset(Qz[i][:], 0.0), writes=[("Qz", i)])
            ctx_tiles = list(range(c.NL, c.NT))
            lat_tiles = list(range(c.NL))
            it = 0
            for d in range(2):
                order = (ctx_tiles + lat_tiles) if d == 0 else (ctx_tiles[::-1] + lat_tiles[::-1])
                P.dve(lambda e: e.memset(Sst[:], 0.0), reads=[], writes=["Sst"])
                P.dve(lambda e: e.memset(Sbf[:], 0.0), reads=[], writes=["Sbf"])
                for t in order:
                    b = it % 2
                    it += 1
                    rows = slice(t * 128, (t + 1) * 128)
                    P.dma(qk[b][:], self.gqk[rows, :], writes=[("qk", b)])
                    P.dma(vb[b][:], self.gv[rows, :], writes=[("vb", b)])
                    P.dma(gl[b][:], self.glog[rows, d * 256:(d + 1) * 256], writes=[("gl", b)])
                    if d == 1:
                        P.dma(ofl[b][:], self.gof[rows, :], reads=[("gof", t)], writes=[("ofl", b)])
                        P.dma(grl[b][:], self.gr[rows, :], writes=[("grl", b)])
                    psB, kB = self.nps()
                    tri_i = 0 if d == 0 else 1
                    P.dve(lambda e, b=b: e.tensor_copy(out=ghl[b][:, 0, :], in_=gl[b][:]), reads=[("gl", b)], writes=[("ghl", b)])
                    P.dve(lambda e, b=b: e.tensor_tensor(out=ghl[b][:, 1, :], in0=gl[b][:], in1=ghl[b][:, 0, :], op=ALU.subtract),
                          reads=[("gl", b), ("ghl", b)], writes=[("ghl", b)])
                    for hl_ in range(2):
                        P.pe(lambda e, psB=psB, b=b, tri_i=tri_i, hl_=hl_: e.matmul(psB[:, 0:256], lhsT=trib[:, tri_i, :], rhs=ghl[b][:, hl_, :],
                                                                                 start=(hl_ == 0), stop=(hl_ == 1)),
                             reads=["trib", ("ghl", b)], writes=[kB])
                    for hl_ in range(2):
                        P.pe(lambda e, psB=psB, b=b, hl_=hl_: e.matmul(psB[:, 256:512], lhsT=onesb[:], rhs=ghl[b][:, hl_, :], start=(hl_ == 0), stop=(hl_ == 1)),
                             reads=["onesb", ("ghl", b)], writes=[kB])
                    psD, kD = self.nps()
                    for p in range(2):
                        for hl_ in range(2):
                            P.pe(lambda e, psD=psD, b=b, p=p, hl_=hl_: e.matmul(psD[:, 16 * p:16 * p + 16], lhsT=ghl[b][:, hl_, p * 128:(p + 1) * 128], rhs=onesb[:, 0:16],
                                                                             start=(hl_ == 0), stop=(hl_ == 1)),
                                 reads=["onesb", ("ghl", b)], writes=[kD])
                    P.act(lambda e, psB=psB: e.activation(out=eb[:], in_=psB[:, 0:256], func=AF.Exp), reads=[kB], writes=["eb"])
                    P.act(lambda e, psB=psB: e.activation(out=enb[:], in_=psB[:, 0:256], func=AF.Exp, scale=-1.0), reads=[kB], writes=["enb"])
                    P.act(lambda e, psB=psB: e.activation(out=ebl[:], in_=psB[:, 256:512], func=AF.Exp), reads=[kB], writes=["ebl"])
                    P.act(lambda e, psD=psD: e.activation(out=dec[:], in_=psD[:, 0:32], func=AF.Exp), reads=[kD], writes=["dec"])
                    dbg = getattr(c, "gla_dbg", 99)
                    if dbg <= 1:
                        continue
                    P.dve(lambda e, b=b: e.tensor_tensor(out=Qt[:], in0=qk[b][:, 0:256], in1=eb[:], op=ALU.mult), reads=[("qk", b), "eb"], writes=["Qt"])
                    P.dve(lambda e, b=b: e.tensor_tensor(out=Kt[:], in0=qk[b][:, 256:512], in1=enb[:], op=ALU.mult), reads=[("qk", b), "enb"], writes=["Kt"])
                    P.dve(lambda e: e.tensor_tensor(out=ebl[:], in0=ebl[:], in1=enb[:], op=ALU.mult), reads=["ebl", "enb"], writes=["ebl"])
                    P.dve(lambda e, b=b: e.tensor_tensor(out=Kd[:], in0=qk[b][:, 256:512], in1=ebl[:], op=ALU.mult), reads=[("qk", b), "ebl"], writes=["Kd"])
                    for j in range(2):
                        P.pe(lambda e, j=j: e.transpose(out=self.pT[:, j * 128:(j + 1) * 128], in_=Qt[:, j * 128:(j + 1) * 128], identity=identb[:]),
                             reads=["Qt", "identb"], writes=["pT"])
                    for j in range(2):
                        P.pe(lambda e, j=j: e.transpose(out=self.pT[:, (2 + j) * 128:(3 + j) * 128], in_=Kt[:, j * 128:(j + 1) * 128], identity=identb[:]),
                             reads=["Kt", "identb"], writes=["pT"])
                    P.act(lambda e, b=b: e.copy(out=QKT[b][:], in_=self.pT[:, 0:512].rearrange("p (k n) -> p k n", k=4)), reads=["pT"], writes=[("QKT", b)])
                    for hh in range(2):
                        P.act(lambda e, b=b, hh=hh: e.copy(out=Qz[b][hh * 64:(hh + 1) * 64, :, :].rearrange("p (a two) n -> p a two n", two=2)[:, :, hh, :],
                                                           in_=self.pT[hh * 64:(hh + 1) * 64, 0:256].rearrange("p (a n) -> p a n", a=2)),
                              reads=["pT"], writes=[("Qz", b)])
                    if dbg <= 2:
                        continue
                    psA, kA = self.nps()
                    for h in range(4):
                        p, hp = h // 2, (h % 2) * 64
                        P.pe(lambda e, psA=psA, h=h, p=p, hp=hp, b=b: e.matmul(psA[:, h * 128:(h + 1) * 128], lhsT=QKT[b][:, 2 + p, :], rhs=Qz[b][:, h, :],
                                                                           start=True, stop=True),
                             reads=[("QKT", b), ("Qz", b)], writes=[kA])
                    P.dve(lambda e, psA=psA, d=d: e.tensor_tensor(out=attT[:], in0=psA[:, :], in1=mask4[d][:].rearrange("p a n -> p (a n)"), op=ALU.mult),
                          reads=[kA, "mask4"], writes=["attT"])
                    if dbg <= 3:
                        continue
                    psO, kO = self.nps()
                    for h in range(4):
                        p, hp = h // 2, (h % 2) * 64
                        P.pe(lambda e, psO=psO, h=h, b=b: e.matmul(psO[:, h * 128:(h + 1) * 128], lhsT=attT[:, h * 128:(h + 1) * 128], rhs=vb[b][:, h * 128:(h + 1) * 128],
                                                               start=True, stop=False),
                             reads=["attT", ("vb", b)], writes=[kO])
                        P.pe(lambda e, psO=psO, h=h, p=p, hp=hp, b=b: e.matmul(psO[:, h * 128:(h + 1) * 128], lhsT=QKT[b][:, p, :], rhs=Sbf[:, h, :],
                                                                           start=False, stop=True),
                             reads=[("QKT", b), "Sbf"], writes=[kO])
                    if dbg <= 4:
                        continue
                    psS, kS = self.nps()
                    for p in range(2):
                        P.pe(lambda e, psS=psS, p=p, b=b: e.matmul(psS[:, p * 256:(p + 1) * 256], lhsT=Kd[:, p * 128:(p + 1) * 128], rhs=vb[b][:, p * 256:(p + 1) * 256],
                                                               start=True, stop=True),
                             reads=["Kd", ("vb", b)], writes=[kS])
                    for p in range(2):
                        for hh in range(2):
                            hp = hh * 64
                            P.dve(lambda e, psS=psS, p=p, hh=hh, hp=hp: e.scalar_tensor_tensor(
                                out=Sst[hp:hp + 64, p, :], in0=Sst[hp:hp + 64, p, :], scalar=dec[hp:hp + 64, 16 * p:16 * p + 1],
                                in1=psS[hp:hp + 64, p * 256 + hh * 128:p * 256 + (hh + 1) * 128], op0=ALU.mult, op1=ALU.add),
                                reads=[kS, "dec", "Sst"], writes=["Sst"])
                    for hh in range(2):
                        P.act(lambda e, hh=hh: e.copy(out=Sbf[hh * 64:(hh + 1) * 64, :, :].rearrange("p (a two) n -> p a two n", two=2)[:, :, hh, :],
                                                     in_=Sst[hh * 64:(hh + 1) * 64, :, :]), reads=["Sst"], writes=["Sbf"])
                    if d == 0:
                        P.dve(lambda e, psO=psO, b=b: e.tensor_copy(out=osb[b][:], in_=psO[:, :]), reads=[kO], writes=[("osb", b)])
                        P.dma(self.gof[rows, :], osb[b][:], reads=[("osb", b)], writes=[("gof", t)], q="pool")
                    else:
                        o = osb[b]
                        P.dve(lambda e, psO=psO, b=b: e.tensor_tensor(out=osb[b][:], in0=psO[:, :], in1=ofl[b][:], op=ALU.add),
                              reads=[kO, ("ofl", b)], writes=[("osb", b)])
                        for h in range(4):
                            P.act(lambda e, h=h, b=b: e.activation(out=junk[:], in_=osb[b][:, h * 128:(h + 1) * 128], func=AF.Square, accum_out=sq[:, h:h + 1]),
                                  reads=[("osb", b)], writes=["junk", "sq"])
                        P.act(lambda e: e.activation(out=sq[:, 4:8], in_=sq[:, 0:4], func=AF.Sqrt, bias=EPS, scale=1.0 / 128), reads=["sq"], writes=["sq"])
                        P.dve(lambda e: e.reciprocal(out=sq[:, 4:8], in_=sq[:, 4:8]), reads=["sq"], writes=["sq"])
                        for h in range(4):
                            P.dve(lambda e, h=h, b=b: e.scalar_tensor_tensor(out=osb[b][:, h * 128:(h + 1) * 128], in0=osb[b][:, h * 128:(h + 1) * 128],
                                                                           scalar=sq[:, 4 + h:5 + h], in1=gnb[:], op0=ALU.mult, op1=ALU.mult),
                                  reads=[("osb", b), "sq", "gnb"], writes=[("osb", b)])
                        P.dve(lambda e, b=b: e.tensor_tensor(out=yb[b][:], in0=osb[b][:], in1=grl[b][:], op=ALU.mult),
                              reads=[("osb", b), ("grl", b)], writes=[("yb", b)])
                        P.dma(self.ycat[rows, :], yb[b][:], reads=[("yb", b)], writes=[("ycat", t)], q="pool")
            P.emit()

    def phase_mla(self, l):
        nc = self.nc
        c = self.cfg
        NTOK = c.NT * 128
        G = 4
        with ExitStack() as es:
            P = Phase(self.prog)
            sb = lambda name, shape, dt: es.enter_context(self.sbt(name, shape, dt))
            KT = sb("KT", [128, NTOK], BF16)
            KrT = sb("KrT", [128, NTOK], BF16)
            V = sb("V", [128, c.NT, 128], BF16)
            onesb = sb("onesb", [128, 128], BF16)
            QT = [sb("QT%d" % i, [128, G * 128], BF16) for i in range(2)]
            QrT = [sb("QrT%d" % i, [128, G * 128], BF16) for i in range(2)]
            PT = [sb("PT%d" % i, [128, G * 128], BF16) for i in range(3)]
            rden = sb("rden", [128, G * 128], F32)
            Ob = [sb("Ob%d" % i, [128, G * 128], BF16) for i in range(2)]
            P.dve(lambda e: e.memset(onesb[:], 1.0), writes=["onesb"])
            P.dve(lambda e: e.memset(KrT[:], 0.0), writes=["KrT"])
            for i in range(2):
                P.dve(lambda e, i=i: e.memset(QrT[i][:], 0.0), writes=[("QrT", i)])
            P.dma(KrT[0:64, :], self.mkr[:, :], reads=["KrT"], writes=["KrT"])
            gi = 0
            pi = 0
            for h in range(4):
                P.dma(KT[:], self.mk[h], writes=["KT"])
                P.dma(V[:], self.mvv[:, h * 128:(h + 1) * 128].rearrange("(c p) d -> p c d", p=128), writes=["V"])
                for (t0, g) in self.groups(G):
                    n = g * 128
                    c0 = t0 * 128
                    qb = gi % 2
                    gi += 1
                    P.dma(QT[qb][:, 0:n], self.mq[h, 0:128, c0:c0 + n], writes=[("QT", qb)])
                    P.dma(QrT[qb][0:64, 0:n], self.mq[h, 128:192, c0:c0 + n], reads=[("QrT", qb)], writes=[("QrT", qb)])
                    chunks = list(range(c.NT)) if t0 < c.NL else list(range(c.NL, c.NT))
                    psO = self.ps[5]
                    psD = self.ps[6]
                    for ci_, kc in enumerate(chunks):
                        sbk = pi % 2
                        pb = pi % 3
                        pi += 1
                        psS = self.ps[sbk]
                        P.pe(lambda e, psS=psS, kc=kc, qb=qb, n=n: e.matmul(psS[:, 0:n], lhsT=KT[:, kc * 128:(kc + 1) * 128], rhs=QT[qb][:, 0:n], start=True, stop=False),
                             reads=["KT", ("QT", qb)], writes=[("psS", sbk)])
                        P.pe(lambda e, psS=psS, kc=kc, qb=qb, n=n: e.matmul(psS[:, 0:n], lhsT=KrT[:, kc * 128:(kc + 1) * 128], rhs=QrT[qb][:, 0:n], start=False, stop=True),
                             reads=["KrT", ("QrT", qb)], writes=[("psS", sbk)])
                        P.act(lambda e, psS=psS, pb=pb, n=n: e.activation(out=PT[pb][:, 0:n], in_=psS[:, 0:n], func=AF.Exp), reads=[("psS", sbk)], writes=[("PT", pb)])
                        first = ci_ == 0
                        last = ci_ == len(chunks) - 1
                        P.pe(lambda e, kc=kc, pb=pb, n=n, first=first, last=last: e.matmul(psO[:, 0:n], lhsT=V[:, kc, :], rhs=PT[pb][:, 0:n], start=first, stop=last),
                             reads=["V", ("PT", pb)], writes=["psO"])
                        P.pe(lambda e, pb=pb, n=n, first=first, last=last: e.matmul(psD[:, 0:n], lhsT=onesb[:], rhs=PT[pb][:, 0:n], start=first, stop=last),
                             reads=["onesb", ("PT", pb)], writes=["psD"])
                    P.dve(lambda e, n=n: e.reciprocal(out=rden[:, 0:n], in_=psD[:, 0:n]), reads=["psD"], writes=["rden"])
                    P.dve(lambda e, n=n, qb=qb: e.tensor_tensor(out=Ob[qb][:, 0:n], in0=psO[:, 0:n], in1=rden[:, 0:n], op=ALU.mult),
                          reads=["psO", "rden"], writes=[("Ob", qb)])
                    P.dma(self.mo[h, :, c0:c0 + n], Ob[qb][:, 0:n], reads=[("Ob", qb)], writes=[("mo", h, c0)], q="pool")
            P.emit()

    def phase_even_out(self, l):
        nc = self.nc
        c = self.cfg
        ie = l // 2
        self._ci = 0
        with ExitStack() as es:
            P = Phase(self.prog)
            sb = lambda name, shape, dt: es.enter_context(self.sbt(name, shape, dt))
            wout = sb("wout", [128, KC, D], BF16)
            stg = [sb("stg%d" % i, [128, D], F32) for i in range(2)]
            identf = sb("identf", [128, 128], F32)
            identb = sb("identb", [128, 128], BF16)
            yin = [sb("yin%d" % i, [128, 512], BF16) for i in range(2)]
            gT = [sb("gT%d" % i, [128, 4, 128], BF16) for i in range(2)]
            mT = [sb("mT%d" % i, [128, 4, 128], BF16) for i in range(2)]
            bc = [sb("bc%d" % i, [128, D], F32) for i in range(1)]
            xin = [sb("xin%d" % i, [128, D], F32) for i in range(2)]
            z = sb("z", [128, D], F32)
            st = sb("st", [128, 2, 6], F32)
            mv = sb("mv", [128, 2], F32)
            rs = sb("rs", [128, 1], F32)
            self.load_ident(P, identf, identb)
            w_d = self.even_w_out[ie]
            self.load_cast(P, stg, lambda i: wout[:, i, :], lambda i: w_d[i * 128:(i + 1) * 128, :], 8, "wout", D)
            cur_r = None
            for t in range(c.NT):
                r_ = 0 if t < c.NL else 1
                if r_ != cur_r:
                    P.dma(bc[0][:], self.modd[2 * l + r_:2 * l + r_ + 1, 5 * D:6 * D].partition_broadcast(128), writes=[("bc", 0)])
                    cur_r = r_
                b = t % 2
                xb = xin[b]
                xk = ("xin", b)
                rows = slice(t * 128, (t + 1) * 128)
                P.dma(xb[:], self.xres[rows, :], reads=[("xres", t)], writes=[xk])
                P.dma(yin[b][:], self.ycat[rows, :], writes=[("yin", b)])
                P.dma(mT[b][:], self.mo.rearrange("h p n -> p h n")[:, :, t * 128:(t + 1) * 128], writes=[("mT", b)])
                for j in range(4):
                    P.pe(lambda e, j=j, b=b: e.transpose(out=self.pT[:, j * 128:(j + 1) * 128], in_=yin[b][:, j * 128:(j + 1) * 128], identity=identb[:]),
                         reads=[("yin", b), "identb"], writes=["pT"])
                P.act(lambda e, b=b: e.copy(out=gT[b][:], in_=self.pT[:, 0:512].rearrange("p (k n) -> p k n", k=4)), reads=["pT"], writes=[("gT", b)])
                for nh in range(2):
                    psy = self.ps[4 + nh]
                    for j in range(8):
                        lh = gT[b][:, j, :] if j < 4 else mT[b][:, j - 4, :]
                        P.pe(lambda e, lh=lh, j=j, nh=nh, psy=psy: e.matmul(psy[:, :], lhsT=lh, rhs=wout[:, j, nh * 512:(nh + 1) * 512], start=(j == 0), stop=(j == 7)),
                             reads=[("gT", b), ("mT", b), "wout"], writes=[("psy", nh)])
                    P.dve(lambda e, nh=nh, psy=psy: e.tensor_tensor(out=z[:, nh * 512:(nh + 1) * 512], in0=psy[:, :], in1=bc[0][:, nh * 512:(nh + 1) * 512], op=ALU.mult),
                          reads=[("psy", nh), ("bc", 0)], writes=["z"])
                self.post_norm(P, z, xb, xk, st, mv, rs, t)
            P.emit()

    def post_norm(self, P, z, xb, xk, st, mv, rs, t, out_ap=None):
        c = self.cfg
        A = c.ALPHA
        P.dve(lambda e: e.scalar_tensor_tensor(out=z[:], in0=xb[:], scalar=A, in1=z[:], op0=ALU.mult, op1=ALU.add),
              reads=[xk, "z"], writes=["z"])
        for h in range(2):
            P.dve(lambda e, h=h: e.bn_stats(out=st[:, h, :], in_=z[:, h * 512:(h + 1) * 512]), reads=["z"], writes=["st"])
        P.dve(lambda e: e.bn_aggr(out=mv[:], in_=st[:].rearrange("p a b -> p (a b)")), reads=["st"], writes=["mv"])
        P.act(lambda e: e.activation(out=rs[:], in_=mv[:, 1:2], func=AF.Sqrt, bias=EPS, scale=1.0), reads=["mv"], writes=["rs"])
        P.dve(lambda e: e.reciprocal(out=rs[:], in_=rs[:]), reads=["rs"], writes=["rs"])
        P.dve(lambda e: e.tensor_scalar(out=xb[:], in0=z[:], scalar1=mv[:, 0:1], scalar2=rs[:, 0:1], op0=ALU.subtract, op1=ALU.mult),
              reads=["z", "mv", "rs"], writes=[xk])
        dst = self.xres[t * 128:(t + 1) * 128, :] if out_ap is None else out_ap
        P.dma(dst, xb[:], reads=[xk], writes=[("xres", t)], q="pool")

    def phase_final(self):
        nc = self.nc
        c = self.cfg
        with ExitStack() as es:
            P = Phase(self.prog)
            xb = [es.enter_context(self.sbt("fb%d" % i, [128, D], F32)) for i in range(2)]
            for t in range(c.NL):
                b = t % 2
                P.dma(xb[b][:], self.xres[t * 128:(t + 1) * 128, :], writes=[("fb", b)])
                P.dma(self.out[t * 128:(t + 1) * 128, :], xb[b][:], reads=[("fb", b)], writes=[("out", t)], q="pool")
            P.emit()


def host_consts(cfg):
    j = np.arange(128)[:, None]
    i = np.arange(128)[None, :]
    tri = np.stack([(j <= i), (j >= i), (j > i)]).astype(np.float32)
    t = np.arange(cfg.S)
    row = (t // GRID_W).astype(np.float32)
    col = (t % GRID_W).astype(np.float32)
    inv = (np.float32(10000.0) ** (-np.arange(16, dtype=np.float32) / np.float32(16))).astype(np.float32)
    ang = np.concatenate([row[:, None] * inv, col[:, None] * inv], axis=-1).astype(np.float32)
    cos = np.cos(ang).astype(np.float32)
    sin = np.sin(ang).astype(np.float32)
    NTOK = cfg.NT * 128
    cosT = np.ones((64, NTOK), np.float32)
    sinT = np.zeros((64, NTOK), np.float32)
    cosT[:, :cfg.S] = np.repeat(cos.T, 2, axis=0)
    sinT[:, :cfg.S] = np.repeat(sin.T, 2, axis=0)
    return {"ident": np.eye(128, dtype=np.float32), "tri": tri, "cosT": cosT, "sinT": sinT}


def build_natab(rpb):
    n_odd = rpb.shape[0]
    tab = np.full((n_odd, 8, 16, 64, 8, 64), -30000.0, dtype=np.float32)
    qc = np.arange(64)
    cs = np.clip(qc - 8, 0, 64 - 16)
    for cls in range(8):
        for rr in range(8):
            rel_row = rr - cls + 7
            for w in range(16):
                kc = cs + w
                rel_col = kc - qc + 15
                tab[:, cls, :, qc, rr, kc] = np.transpose(rpb[:, :, rel_row, rel_col], (2, 0, 1))
    return tab.reshape(n_odd, 8, 16, 64, 512)


def make_in_map(cfg, b, inputs):
    cc = np.stack([np.asarray(inputs["c"])[b], np.asarray(inputs["c_ctx"])], axis=0)
    cc = np.ascontiguousarray(cc.reshape(2, KC, 128).transpose(2, 1, 0).reshape(128, 16)).astype(np.float32)
    m = {
        "x": np.ascontiguousarray(np.asarray(inputs["x"])[b]),
        "ctx": np.ascontiguousarray(np.asarray(inputs["ctx"])[b]),
        "cc": cc,
    }
    m.update(host_consts(cfg))
    n_odd = max(cfg.DEPTH // 2, 1)
    n_even = (cfg.DEPTH + 1) // 2
    m["natab"] = build_natab(np.asarray(inputs["na_rpb"])[:n_odd])
    for k in ("ada_w", "ada_b", "ffn1_w_in", "ffn1_w_out", "ffn2_w_in", "ffn2_w_out"):
        m[k] = np.ascontiguousarray(np.asarray(inputs[k])[:cfg.DEPTH])
    for k in ("na_w_in", "na_w_out"):
        m[k] = np.ascontiguousarray(np.asarray(inputs[k])[:n_odd])
    for k in ("even_w_in", "gla_wg2_f", "gla_wg2_b", "gla_bg_f", "gla_bg_b", "gla_norm_g", "mla_q_norm_g", "mla_kv_norm_g",
              "mla_w_uq", "mla_w_ukv", "even_w_out"):
        m[k] = np.ascontiguousarray(np.asarray(inputs[k])[:n_even])
    return m


def kernel(**inputs):
    B, S, _ = inputs["x"].shape
    CTX = inputs["ctx"].shape[1]
    DEPTH = inputs["ada_w"].shape[0]
    cfg = Cfg(S, CTX, DEPTH)
    kb = K(cfg)
    nc = kb.build()
    in_maps = [make_in_map(cfg, b, inputs) for b in range(B)]
    res = run_bass_kernel_spmd(nc, in_maps, core_ids=list(range(B)))
    return np.stack([r["out"] for r in res.results], axis=0)
```
